# Optimizing a Trainium2 kernel written in Bass

```python
import jax, jax.numpy as jnp
from jax import lax
import numpy as np

D_MODEL = 2048
BATCH = 4
SEQ = 4096
DEPTH = 2

RMS_EPS = 1e-6
CONV_WIDTH = 4
LRU_WIDTH = D_MODEL
LRU_BLOCKS = 16
LRU_BLOCK = LRU_WIDTH // LRU_BLOCKS
LRU_C = 8.0
MLSTM_WIDTH = D_MODEL
MLSTM_HEADS = 8
MLSTM_HEAD_DIM = MLSTM_WIDTH // MLSTM_HEADS
MLSTM_QKV_BLOCK = 4
MLSTM_QKV_BLOCKS = MLSTM_WIDTH // MLSTM_QKV_BLOCK
MLSTM_CHUNK = 128
MLSTM_GN_EPS = 1e-6
EVEN_MIX = LRU_WIDTH + MLSTM_WIDTH
RWKV_WIDTH = 2 * D_MODEL
RWKV_HEAD_DIM = 64
RWKV_HEADS = RWKV_WIDTH // RWKV_HEAD_DIM
RWKV_DECAY_RANK = 96
RWKV_A_RANK = 96
RWKV_GN_EPS = 64e-5

kernel_name = 'hybrid_rglru_mlstm_rwkv7'


def rms_norm(x, g):
    xf = x.astype(jnp.float32)
    y = xf * lax.rsqrt(jnp.mean(xf * xf, axis=-1, keepdims=True) + RMS_EPS)
    return (y * g.astype(jnp.float32)).astype(x.dtype)


def head_norm(x, n_heads, eps, g, b=None):
    xf = x.astype(jnp.float32)
    xh = xf.reshape(*xf.shape[:-1], n_heads, -1)
    mu = jnp.mean(xh, axis=-1, keepdims=True)
    var = jnp.mean(jnp.square(xh - mu), axis=-1, keepdims=True)
    y = ((xh - mu) * lax.rsqrt(var + eps)).reshape(xf.shape) * g.astype(jnp.float32)
    if b is not None:
        y = y + b.astype(jnp.float32)
    return y.astype(x.dtype)


def shift_right(x):
    return jnp.pad(x[:, :-1], ((0, 0), (1, 0), (0, 0)))


def causal_dwconv(x, w, b):
    K = w.shape[0]
    S = x.shape[1]
    xp = jnp.pad(x, ((0, 0), (K - 1, 0), (0, 0)))
    y = b
    for j in range(K):
        y = y + xp[:, j:j + S] * w[j]
    return y


def block_diag(x, w):
    nb, bs, bo = w.shape
    xb = x.reshape(*x.shape[:-1], nb, bs)
    return jnp.einsum('bsni,nij->bsnj', xb, w).reshape(x.shape[:-1] + (nb * bo,))


def rg_lru(x, w_a, b_a, w_x, b_x, lam):
    xf = x.astype(jnp.float32)
    r = jax.nn.sigmoid(block_diag(xf, w_a.astype(jnp.float32)) + b_a.astype(jnp.float32))
    i = jax.nn.sigmoid(block_diag(xf, w_x.astype(jnp.float32)) + b_x.astype(jnp.float32))
    log_a = -LRU_C * r * jax.nn.softplus(-lam.astype(jnp.float32))
    a = jnp.exp(log_a)
    u = jnp.sqrt(-jnp.expm1(2.0 * log_a)) * (i * xf)

    def combine(c1, c2):
        a1, b1 = c1
        a2, b2 = c2
        return a1 * a2, a2 * b1 + b2

    _, h = lax.associative_scan(combine, (a, u), axis=1)
    return h.astype(x.dtype)


def mlstm_chunkwise(q, k, v, ig, fg):
    B, S, _ = q.shape
    H, d, L = MLSTM_HEADS, MLSTM_HEAD_DIM, MLSTM_CHUNK
    nc = S // L

    def to_chunks(t):
        return t.astype(jnp.float32).reshape(B, nc, L, H, d).transpose(1, 0, 3, 2, 4)

    def gate_chunks(t):
        return t.astype(jnp.float32).reshape(B, nc, L, H).transpose(1, 0, 3, 2)

    qc = to_chunks(q)
    kc = to_chunks(k) * (d ** -0.5)
    vc = to_chunks(v)
    igc = gate_chunks(ig)
    lfc = jax.nn.log_sigmoid(gate_chunks(fg))
    causal = jnp.tril(jnp.ones((L, L), dtype=bool))

    def body(carry, inp):
        C, n, m = carry
        qb, kb, vb, ib, lb = inp
        b = jnp.cumsum(lb, axis=-1)
        Dm = b[..., :, None] - b[..., None, :] + ib[..., None, :]
        Dm = jnp.where(causal, Dm, -jnp.inf)
        m_inter = b + m[..., None]
        m_t = jnp.maximum(jnp.max(Dm, axis=-1), m_inter)
        scores = jnp.einsum('bhtd,bhsd->bhts', qb, kb) * jnp.exp(Dm - m_t[..., None])
        inter = jnp.exp(m_inter - m_t)
        num = (jnp.einsum('bhts,bhsd->bhtd', scores, vb)
               + inter[..., None] * jnp.einsum('bhtd,bhde->bhte', qb, C))
        den = jnp.sum(scores, axis=-1) + inter * jnp.einsum('bhtd,bhd->bht', qb, n)
        h = num / jnp.maximum(jnp.abs(den), jnp.exp(-m_t))[..., None]
        bL = b[..., -1]
        g = bL[..., None] - b + ib
        m_new = jnp.maximum(bL + m, jnp.max(g, axis=-1))
        decay = jnp.exp(bL + m - m_new)
        wts = jnp.exp(g - m_new[..., None])
        C = decay[..., None, None] * C + jnp.einsum('bhs,bhsd,bhse->bhde', wts, kb, vb)
        n = decay[..., None] * n + jnp.einsum('bhs,bhsd->bhd', wts, kb)
        return (C, n, m_new), h

    init = (jnp.zeros((B, H, d, d), jnp.float32),
            jnp.zeros((B, H, d), jnp.float32),
            jnp.zeros((B, H), jnp.float32))
    _, h = lax.scan(body, init, (qc, kc, vc, igc, lfc))
    return h.transpose(1, 0, 3, 2, 4).reshape(B, S, H * d).astype(q.dtype)


def rwkv7_scan(r, w, k, v, kk, a):
    B, _, H, d = r.shape

    def step(S, inp):
        r_t, w_t, k_t, v_t, kk_t, a_t = inp
        sa = jnp.einsum('bhvk,bhk->bhv', S, -kk_t)
        S = (S * w_t[:, :, None, :] + sa[..., None] * (kk_t * a_t)[:, :, None, :]
             + v_t[..., None] * k_t[:, :, None, :])
        return S, jnp.einsum('bhvk,bhk->bhv', S, r_t)

    xs = (jnp.moveaxis(r, 1, 0), jnp.moveaxis(w, 1, 0), jnp.moveaxis(k, 1, 0),
          jnp.moveaxis(v, 1, 0), jnp.moveaxis(kk, 1, 0), jnp.moveaxis(a, 1, 0))
    _, y = lax.scan(step, jnp.zeros((B, H, d, d), jnp.float32), xs)
    return jnp.moveaxis(y, 0, 1)


def even_mixer(h, w_in, lru_conv_w, lru_conv_b, lru_wa, lru_ba, lru_wx, lru_bx, lru_lambda,
               m_conv_w, m_conv_b, m_wq, m_wk, m_wv, m_wi, m_bi, m_wf, m_bf, m_skip, m_gn,
               w_out):
    u = h @ w_in
    xr, zr, xm, zm = jnp.split(
        u, [LRU_WIDTH, 2 * LRU_WIDTH, 2 * LRU_WIDTH + MLSTM_WIDTH], axis=-1)
    xr = causal_dwconv(xr, lru_conv_w, lru_conv_b)
    yr = rg_lru(xr, lru_wa, lru_ba, lru_wx, lru_bx, lru_lambda)
    xmc = jax.nn.silu(causal_dwconv(xm, m_conv_w, m_conv_b))
    q = block_diag(xmc, m_wq)
    k = block_diag(xmc, m_wk)
    v = block_diag(xm, m_wv)
    qkv = jnp.concatenate([q, k, v], axis=-1)
    ig = qkv @ m_wi + m_bi
    fg = qkv @ m_wf + m_bf
    ym = mlstm_chunkwise(q, k, v, ig, fg)
    ym = head_norm(ym, MLSTM_HEADS, MLSTM_GN_EPS, m_gn) + m_skip * xmc
    y = jnp.concatenate([yr * jax.nn.silu(zr), ym * jax.nn.silu(zm)], axis=-1)
    return y @ w_out


def odd_mixer(h, w_in, mu_rkv, mu_w, mu_a, w0, w1, w2, a0, a1, a2, k_k, k_a, r_k,
              gn_g, gn_b, w_out):
    B, S, _ = h.shape
    H, d = RWKV_HEADS, RWKV_HEAD_DIM
    u = h @ w_in
    rkv, z = u[..., :3 * RWKV_WIDTH], u[..., 3 * RWKV_WIDTH:]
    rkv = rkv + (shift_right(rkv) - rkv) * mu_rkv
    r, k, v = jnp.split(rkv, 3, axis=-1)
    dh = shift_right(h) - h
    xw = h + dh * mu_w
    xa = h + dh * mu_a
    w_log = -jax.nn.softplus(-(w0 + jnp.tanh(xw @ w1) @ w2)) - 0.5
    decay = jnp.exp(-jnp.exp(w_log.astype(jnp.float32)))
    a = jax.nn.sigmoid((a0 + (xa @ a1) @ a2).astype(jnp.float32))
    kk = (k * k_k).astype(jnp.float32).reshape(B, S, H, d)
    kk = kk / jnp.maximum(jnp.sqrt(jnp.sum(kk * kk, axis=-1, keepdims=True)), 1e-12)
    k = k.astype(jnp.float32) * (1.0 + (a - 1.0) * k_a.astype(jnp.float32))
    rh = r.astype(jnp.float32).reshape(B, S, H, d)
    kh = k.reshape(B, S, H, d)
    vh = v.astype(jnp.float32).reshape(B, S, H, d)
    ah = a.reshape(B, S, H, d)
    wh = decay.reshape(B, S, H, d)
    y = rwkv7_scan(rh, wh, kh, vh, kk, ah)
    y = head_norm(y.reshape(B, S, RWKV_WIDTH), H, RWKV_GN_EPS, gn_g, gn_b)
    bonus = jnp.sum(rh * kh * r_k.astype(jnp.float32), axis=-1, keepdims=True) * vh
    y = (y + bonus.reshape(B, S, RWKV_WIDTH)).astype(h.dtype)
    return (y * jax.nn.silu(z)) @ w_out


def setup_inputs(seed: int = 0) -> dict:
    key = jax.random.key(seed)
    ks = iter(jax.random.split(key, 64))
    f32 = jnp.float32

    def nrm(shape, scale):
        return jax.random.normal(next(ks), shape, f32) * scale

    def gain(n):
        return 1.0 + nrm((n,), 0.02)

    def unif(shape, lo, hi):
        return jax.random.uniform(next(ks), shape, f32, lo, hi)

    D, R, M, W = D_MODEL, LRU_WIDTH, MLSTM_WIDTH, RWKV_WIDTH
    x = jax.random.normal(next(ks), (BATCH, SEQ, D), f32)
    s = unif((R,), 0.9, 0.999) ** (1.0 / LRU_C)
    lru_lambda = jnp.log(s) - jnp.log1p(-s)
    w0 = jnp.tile(jnp.linspace(-6.0, -1.0, RWKV_HEAD_DIM, dtype=f32), RWKV_HEADS) + nrm((W,), 0.1)
    return {
        'x': x,
        'l0_norm_pre': gain(D),
        'l0_w_in': nrm((D, 2 * EVEN_MIX), D ** -0.5),
        'l0_lru_conv_w': nrm((CONV_WIDTH, R), CONV_WIDTH ** -0.5),
        'l0_lru_conv_b': nrm((R,), 0.01),
        'l0_lru_wa': nrm((LRU_BLOCKS, LRU_BLOCK, LRU_BLOCK), LRU_BLOCK ** -0.5),
        'l0_lru_ba': nrm((R,), 0.01),
        'l0_lru_wx': nrm((LRU_BLOCKS, LRU_BLOCK, LRU_BLOCK), LRU_BLOCK ** -0.5),
        'l0_lru_bx': nrm((R,), 0.01),
        'l0_lru_lambda': lru_lambda,
        'l0_m_conv_w': nrm((CONV_WIDTH, M), CONV_WIDTH ** -0.5),
        'l0_m_conv_b': nrm((M,), 0.01),
        'l0_m_wq': nrm((MLSTM_QKV_BLOCKS, MLSTM_QKV_BLOCK, MLSTM_QKV_BLOCK), MLSTM_QKV_BLOCK ** -0.5),
        'l0_m_wk': nrm((MLSTM_QKV_BLOCKS, MLSTM_QKV_BLOCK, MLSTM_QKV_BLOCK), MLSTM_QKV_BLOCK ** -0.5),
        'l0_m_wv': nrm((MLSTM_QKV_BLOCKS, MLSTM_QKV_BLOCK, MLSTM_QKV_BLOCK), MLSTM_QKV_BLOCK ** -0.5),
        'l0_m_wi': nrm((3 * M, MLSTM_HEADS), (3 * M) ** -0.5),
        'l0_m_bi': nrm((MLSTM_HEADS,), 0.1),
        'l0_m_wf': nrm((3 * M, MLSTM_HEADS), (3 * M) ** -0.5),
        'l0_m_bf': jnp.linspace(3.0, 6.0, MLSTM_HEADS, dtype=f32) + nrm((MLSTM_HEADS,), 0.1),
        'l0_m_skip': gain(M),
        'l0_m_gn': gain(M),
        'l0_w_out': nrm((EVEN_MIX, D), EVEN_MIX ** -0.5),
        'l0_norm_post': gain(D),
        'l1_norm_pre': gain(D),
        'l1_w_in': nrm((D, 4 * W), D ** -0.5),
        'l1_mu_rkv': unif((3 * W,), 0.0, 1.0),
        'l1_mu_w': unif((D,), 0.0, 1.0),
        'l1_mu_a': unif((D,), 0.0, 1.0),
        'l1_w0': w0,
        'l1_w1': nrm((D, RWKV_DECAY_RANK), D ** -0.5),
        'l1_w2': nrm((RWKV_DECAY_RANK, W), 0.1 * RWKV_DECAY_RANK ** -0.5),
        'l1_a0': nrm((W,), 0.1),
        'l1_a1': nrm((D, RWKV_A_RANK), D ** -0.5),
        'l1_a2': nrm((RWKV_A_RANK, W), RWKV_A_RANK ** -0.5),
        'l1_k_k': 0.85 + nrm((W,), 0.02),
        'l1_k_a': 1.0 + nrm((W,), 0.02),
        'l1_r_k': nrm((RWKV_HEADS, RWKV_HEAD_DIM), 0.1),
        'l1_gn_g': gain(W),
        'l1_gn_b': nrm((W,), 0.02),
        'l1_w_out': nrm((W, D), W ** -0.5),
        'l1_norm_post': gain(D),
    }


def reference(x, l0_norm_pre, l0_w_in, l0_lru_conv_w, l0_lru_conv_b, l0_lru_wa, l0_lru_ba,
              l0_lru_wx, l0_lru_bx, l0_lru_lambda, l0_m_conv_w, l0_m_conv_b, l0_m_wq, l0_m_wk,
              l0_m_wv, l0_m_wi, l0_m_bi, l0_m_wf, l0_m_bf, l0_m_skip, l0_m_gn, l0_w_out,
              l0_norm_post, l1_norm_pre, l1_w_in, l1_mu_rkv, l1_mu_w, l1_mu_a, l1_w0, l1_w1,
              l1_w2, l1_a0, l1_a1, l1_a2, l1_k_k, l1_k_a, l1_r_k, l1_gn_g, l1_gn_b, l1_w_out,
              l1_norm_post):
    even_p = (l0_w_in, l0_lru_conv_w, l0_lru_conv_b, l0_lru_wa, l0_lru_ba, l0_lru_wx,
              l0_lru_bx, l0_lru_lambda, l0_m_conv_w, l0_m_conv_b, l0_m_wq, l0_m_wk, l0_m_wv,
              l0_m_wi, l0_m_bi, l0_m_wf, l0_m_bf, l0_m_skip, l0_m_gn, l0_w_out)
    odd_p = (l1_w_in, l1_mu_rkv, l1_mu_w, l1_mu_a, l1_w0, l1_w1, l1_w2, l1_a0, l1_a1, l1_a2,
             l1_k_k, l1_k_a, l1_r_k, l1_gn_g, l1_gn_b, l1_w_out)
    layer_params = [(l0_norm_pre, l0_norm_post, even_p), (l1_norm_pre, l1_norm_post, odd_p)]
    for layer in range(DEPTH):
        g_pre, g_post, p = layer_params[layer]
        h = rms_norm(x, g_pre)
        h = even_mixer(h, *p) if layer % 2 == 0 else odd_mixer(h, *p)
        x = x + rms_norm(h, g_post)
    return x
```

```python
import numpy as np
from contextlib import ExitStack
import concourse.bass as bass
import concourse.mybir as mybir
from concourse.bass_utils import run_bass_kernel_spmd

F32 = mybir.dt.float32
BF16 = mybir.dt.bfloat16
AF = mybir.ActivationFunctionType
ALU = mybir.AluOpType
AX = mybir.AxisListType

EPOCH = 20000
NO_POOL = True
SYNC_SAME_ENGINE_WAW = False
PE_DRAIN_ON_MODE_SWITCH = True
BF16_ONLY = True
CE = ("pe", "act", "dve", "pool")


class _Rec:
    def __init__(self):
        self.call = None

    def __getattr__(self, name):
        def f(*a, **kw):
            self.call = (name, a, kw)
            return self
        return f


class K:
    def __init__(self):
        self.nc = bass.Bass("TRN2", target_bir_lowering=False)
        self.es = ExitStack()
        nc = self.nc
        self.eng = {"pe": nc.tensor, "act": nc.scalar, "dve": nc.vector, "pool": nc.gpsimd, "sp": nc.sync}
        self.semh = {}
        self.idx = {e: 0 for e in CE}
        self.seen = {e: {} for e in self.eng}
        self.clock = {}
        self.last_w = {}
        self.readers = {}
        self.dma_sem = {}
        self.prog = []
        self.needed = set()
        self.n_wait = 0
        self.pfx = ""
        self.scopes = []
        self._emitted = 0
        self._cnt = {e: 0 for e in CE}
        self._tr = {}
        self.n_inc = 0

    def _sem(self, name):
        h = self.es.enter_context(self.nc.semaphore(name))
        self.semh[name] = h
        return h

    def _stack(self):
        return self.scopes[-1] if self.scopes else self.es

    def sb(self, name, shape, dt=F32):
        return self._stack().enter_context(self.nc.sbuf_tensor(self.pfx + name, list(shape), dt))

    def ps(self, name, shape, dt=F32):
        return self._stack().enter_context(self.nc.psum_tensor(self.pfx + name, list(shape), dt))

    def dram(self, name, shape, dt=F32, kind="ExternalInput"):
        return self.nc.dram_tensor(self.pfx + name, list(shape), dt, kind=kind).ap()

    def push_scope(self, pfx):
        self.pfx = pfx
        self._hl = {}
        self.scopes.append(ExitStack())

    def pop_scope(self):
        toks = [(e, self.idx[e]) for e in CE if self.idx[e] > 0] + [v for v in self.dma_sem.values() if v[1] > 0]
        for e in self.eng:
            deps = {}
            for s_, v in toks:
                if s_ != e:
                    deps[s_] = v
            self.prog.append(("wait", e, None, self._waits(e, deps), None))
        self.emit()
        self.scopes.pop().close()
        self.pfx = ""

    def _deps(self, e, R, W):
        deps = {}

        def add(tok, raw):
            s, v = tok
            if s == e and not raw and (e == "pe" or not SYNC_SAME_ENGINE_WAW):
                return
            if deps.get(s, 0) < v:
                deps[s] = v

        for k in R:
            if k in self.last_w:
                add(self.last_w[k], True)
            if isinstance(k, str) and k.startswith("ps"):
                for t in self.readers.get(k, ()):
                    add(t, False)
        for k in W:
            if k in self.last_w:
                add(self.last_w[k], False)
            for t in self.readers.get(k, ()):
                add(t, False)
        return deps

    def _waits(self, e, deps):
        seen = self.seen[e]
        out = []
        for s, v in sorted(deps.items()):
            if seen.get(s, 0) >= v:
                continue
            out.append((s, v))
            if s in CE:
                self.needed.add((s, v))
            seen[s] = v
            for s2, v2 in self.clock.get((s, v), {}).items():
                if seen.get(s2, 0) < v2:
                    seen[s2] = v2
        self.n_wait += len(out)
        return out

    def _commit(self, e, tok, R, W):
        self.clock[tok] = dict(self.seen[e])
        for k in R:
            self.readers.setdefault(k, []).append(tok)
        for k in W:
            self.last_w[k] = tok
            self.readers[k] = []

    def op(self, e, fn, R=(), W=()):
        rec = _Rec()
        fn(rec)
        call = rec.call
        if e == "pool" and NO_POOL:
            self._flip = not getattr(self, "_flip", False)
            if call[0] == "tensor_copy" and self._flip:
                e = "act"
                call = ("activation", (), dict(out=call[2]["out"], in_=call[2]["in_"], func=AF.Copy))
            else:
                e = "dve"
        waits = self._waits(e, self._deps(e, R, W))
        self.idx[e] += 1
        tok = (e, self.idx[e])
        self.prog.append(("op", e, call, waits, tok))
        self._commit(e, tok, R, W)
        return tok

    def dma(self, out, in_, R=(), W=(), key=None, q="sp"):
        key = key if key is not None else (W[0] if W else R[0])
        if key not in self.dma_sem:
            self.dma_sem[key] = (f"d_{len(self.dma_sem)}", 0)
        s, c = self.dma_sem[key]
        deps = self._deps(q, R, W)
        if c > 0 and deps.get(s, 0) < c:
            deps[s] = c
        waits = self._waits(q, deps)
        c += 16
        self.dma_sem[key] = (s, c)
        tok = (s, c)
        self.prog.append(("dma", q, (out, in_), waits, tok))
        self._commit(q, tok, R, W)
        return tok

    def finish(self, keys, e="sp"):
        deps = {}
        for k in keys:
            s, v = self.last_w[k]
            deps[s] = max(deps.get(s, 0), v)
        self.prog.append(("wait", e, None, self._waits(e, deps), None))

    def barrier(self, keys, engines=CE):
        for e in engines:
            deps = {}
            for k_ in keys:
                s, v = self.last_w[k_]
                deps[s] = max(deps.get(s, 0), v)
            self.prog.append(("wait", e, None, self._waits(e, deps), None))

    def emit(self):
        todo = self.prog[self._emitted:]
        cnt, tr = self._cnt, self._tr
        for kind, e, fn, waits, tok in todo:
            if kind == "op" and tok in self.needed:
                cnt[e] += 1
                ep, c = divmod(cnt[e] - 1, EPOCH)
                tr[tok] = (f"s_{e}_{ep}", c + 1)
        for name in sorted({v[0] for v in tr.values()} | {s for s, _ in self.dma_sem.values()}):
            if name not in self.semh:
                h = self.es.enter_context(self.nc.semaphore(name))
                self.semh[name] = h
        for kind, e, fn, waits, tok in todo:
            for s, v in waits:
                sn, sv = tr[(s, v)] if s in CE else (s, v)
                self.eng[e].wait_ge(self.semh[sn], sv)
            if kind == "op":
                name, a, kw = fn
                if e == "pe" and PE_DRAIN_ON_MODE_SWITCH:
                    st = kw.get("lhsT") if name == "matmul" else (a[1] if len(a) > 1 else kw.get("in_"))
                    shp = list(st.shape)
                    rnd = lambda v: 32 if v <= 32 else (64 if v <= 64 else 128)
                    mode = (rnd(int(shp[0])), rnd(int(np.prod(shp[1:]))))
                    if getattr(self, "_pe_mode", None) not in (None, mode):
                        self.eng["pe"].drain()
                        self.n_drain = getattr(self, "n_drain", 0) + 1
                    self._pe_mode = mode
                ins = getattr(self.eng[e], name)(*a, **kw)
                if tok in tr:
                    sn, sv = tr[tok]
                    ins.then_inc(self.semh[sn], 1)
                    self.n_inc += 1
            elif kind == "dma":
                out, in_ = fn
                self.eng[e].dma_start(out=out, in_=in_).then_inc(self.semh[tok[0]], 16)
        self._emitted = len(self.prog)

    def close(self):
        self.emit()
        self.es.close()

    def stats(self):
        return dict(ins=dict(self.idx), waits=self.n_wait, incs=self.n_inc, sems=len(self.semh), ndma=sum(1 for p in self.prog if p[0] == "dma"), drains=getattr(self, 'n_drain', 0))


TT = 512
NT = 8
S = 4096
LN16 = float(np.log(16.0))
STOP = 0


def _BF():
    return BF16_ONLY


class Rot:
    def __init__(self, k, name, n, shape, dt=F32, ps=False):
        self.t = [(k.ps if ps else k.sb)(f"{name}{i}", shape, dt) for i in range(n)]
        self.name, self.n, self.i = name, n, 0

    def get(self):
        j = self.i % self.n
        self.i += 1
        return self.t[j], f"{self.name}{j}"


def A_(k, out, in_, func, R, W, bias=None, scale=None):
    kw = {}
    if bias is not None:
        kw["bias"] = bias
    if scale is not None:
        kw["scale"] = scale
    return k.op("act", lambda e: e.activation(out=out, in_=in_, func=func, **kw), R, W)


def TS(k, eng, out, in0, s1, op0, R, W, s2=None, op1=None):
    if op1 is None:
        return k.op(eng, lambda e: e.tensor_scalar(out=out, in0=in0, scalar1=s1, scalar2=None, op0=op0), R, W)
    return k.op(eng, lambda e: e.tensor_scalar(out=out, in0=in0, scalar1=s1, scalar2=s2, op0=op0, op1=op1), R, W)


def STT(k, out, in0, scalar, op0, in1, op1, R, W):
    return k.op("dve", lambda e: e.scalar_tensor_tensor(out=out, in0=in0, scalar=scalar, in1=in1, op0=op0, op1=op1), R, W)


def TTo(k, eng, out, in0, in1, op, R, W):
    return k.op(eng, lambda e: e.tensor_tensor(out=out, in0=in0, in1=in1, op=op), R, W)


def MM(k, out, lhsT, rhs, start, stop, R, W):
    return k.op("pe", lambda e: e.matmul(out, lhsT=lhsT, rhs=rhs, start=start, stop=stop), R, W)


def bf_const(k, C, name):
    key = "cb_" + name
    if key not in C:
        t = C[name]
        tb = k.sb(key, list(t.shape), BF16)
        k.op("dve", lambda e: e.tensor_copy(out=tb[:], in_=t[:]), ["c_" + name], [key])
        k.barrier([key])
        C[key] = tb
    return C[key]


def MM32(k, C, out, cname, rhs, start, stop, R, W, precise=True, shape=None):
    if not BF16_ONLY:
        return MM(k, out, C[cname][:, :] if shape is None else C[cname][0:shape[0], 0:shape[1]], rhs, start, stop, R, W)
    cb = bf_const(k, C, cname)
    cb_ap = cb[:, :] if shape is None else cb[0:shape[0], 0:shape[1]]
    npart, nfree = rhs.shape[0], int(np.prod(rhs.shape[1:]))
    if not hasattr(k, "_hl"):
        k._hl = {}
    kk_ = (npart, nfree)
    if kk_ not in k._hl:
        k._hl[kk_] = (Rot(k, f"hi{npart}_{nfree}", 1, [npart, nfree], BF16), Rot(k, f"lo{npart}_{nfree}", 1, [npart, nfree], BF16))
    hi, hik = k._hl[kk_][0].get()
    k.op("dve", lambda e: e.tensor_copy(out=hi[:], in_=rhs), R, [hik])
    if not precise:
        return MM(k, out, cb_ap, hi[:], start, stop, [hik], W)
    lo, lok = k._hl[kk_][1].get()
    k.op("dve", lambda e: e.tensor_tensor(out=lo[:], in0=rhs, in1=hi[:], op=ALU.subtract), list(R) + [hik], [lok])
    MM(k, out, cb_ap, hi[:], start, False, [hik], W)
    return MM(k, out, cb_ap, lo[:], False, stop, [lok], W)


def load_consts(k, names_shapes):
    out = {}
    for name, shape in names_shapes:
        d = k.dram(name, shape)
        t = k.sb("c_" + name, shape)
        k.dma(t[:], d, W=["c_" + name], key="consts")
        out[name] = t
    k.barrier(["c_" + n for n, _ in names_shapes])
    return out


def prenorm(k, C, xT_d, t0, hT, gname, PSn):
    sqr, xr = k._rot_sq, k._rot_x
    ps, psk = PSn.get()
    for c in range(16):
        xt, xk = xr.get()
        k.dma(xt[:], xT_d[:, c, t0:t0 + TT], W=[xk])
        sq, sqk = sqr.get()
        A_(k, sq[:], xt[:], AF.Square, [xk], [sqk])
        MM32(k, C, ps[:, :], "ones", sq[:], c == 0, c == 15, [sqk], [psk], precise=False)
    rstd = k._rstd
    A_(k, rstd[:], ps[:, :], AF.Ln, [psk], ["rstd"], bias=C["eps6"][:, 0:1], scale=1.0 / 2048.0)
    A_(k, rstd[:], rstd[:], AF.Exp, ["rstd"], ["rstd"], scale=-0.5)
    for c in range(16):
        xt, xk = xr.get()
        k.dma(xt[:], xT_d[:, c, t0:t0 + TT], W=[xk])
        STT(k, hT[:, c, :], xt[:], C[gname][:, c:c + 1], ALU.mult, rstd[:], ALU.mult, [xk, "rstd", "c_" + gname], ["hT"])


def build_A(k, xT_d=None, y_d=None, yoff=None):
    nc = k.nc
    if xT_d is None:
        xT_d = k.dram("xT", [128, 16, S])
    win_d = k.dram("win", [40, 128, 16 * 128])
    if y_d is None:
        y_d = k.dram("y0T", [128, 16, S], BF16, kind="ExternalOutput")
        yoff = (0, 8)
    YL, YM = yoff
    names = [("ones", [128, 128]), ("ident", [128, 128]), ("maskle", [128, 128]), ("eps6", [128, 1]),
             ("g0pre", [128, 16]),
             ("lcw", [128, 8 * 4]), ("lcb", [128, 8]), ("lba", [128, 8]), ("lbx", [128, 8]), ("llam", [128, 8]),
             ("mcw", [128, 16 * 4]), ("mcb", [128, 16]), ("mskip", [128, 8]), ("mgn", [128, 8]),
             ("gbias", [36, 1]),
             ("rmask", [36, TT]), ("nmask", [36, TT]), ("eye4", [4, 4]), ("ones4", [128, 128])]
    C = load_consts(k, names)
    lwa_d = k.dram("lwa", [128, 8 * 128]); lwx_d = k.dram("lwx", [128, 8 * 128])
    wq_d = k.dram("wqbd", [128, 16 * 128]); wk_d = k.dram("wkbd", [128, 16 * 128]); wv_d = k.dram("wvbd", [128, 16 * 128])
    wg_d = k.dram("wgate", [3, 128, 16 * 128])
    lwa = k.sb("lwa_s", [128, 8 * 128], BF16); lwx = k.sb("lwx_s", [128, 8 * 128], BF16)
    wq = k.sb("wq_s", [128, 16 * 128], BF16); wk = k.sb("wk_s", [128, 16 * 128], BF16); wv = k.sb("wv_s", [128, 16 * 128], BF16)
    wg = k.sb("wg_s", [128, 48 * 128], BF16)
    k._setup_w = [(lwa[:], lwa_d, "lwa", 1024), (lwx[:], lwx_d, "lwx", 1024), (wq[:], wq_d, "wq", 2048), (wk[:], wk_d, "wk", 2048), (wv[:], wv_d, "wv", 2048)] + [(wg[:, i * 2048:(i + 1) * 2048], wg_d[i], "wg", 2048) for i in range(3)]
    identb = k.sb("identb", [128, 128], BF16)
    k.op("dve", lambda e: e.tensor_copy(out=identb[:], in_=C["ident"][:]), ["c_ident"], ["identb"])

    hT = k.sb("hT", [128, 16, TT], BF16); k._rot_x = Rot(k, "xt", 2, [128, TT])
    k._rot_sq = Rot(k, "sq", 2, [128, TT]); k._rstd = k.sb("rstd", [128, TT])
    wst = Rot(k, "wst", 2, [128, 16 * 128], BF16); wstf = Rot(k, "wstf", 2, [128, 16 * 128])
    for t, d, nm, n in k._setup_w:
        wf, wfkey = wstf.get()
        k.dma(wf[:, 0:n], d, W=[wfkey])
        k.op("pool", lambda en: en.tensor_copy(out=t, in_=wf[:, 0:n]), [wfkey], [nm])
    PSA = Rot(k, "psA", 2, [128, 512], ps=True)
    PSB = Rot(k, "psB", 2, [128, 512], ps=True)
    PSC = Rot(k, "psC", 2, [128, 512], ps=True)
    PSG = k.ps("psG", [128, 512])
    PST = Rot(k, "psT", 1, [128, 256], BF16, ps=True)
    yT = k.sb("yT", [128, 8, TT], BF16)
    lhist = k.sb("lhist", [128, 8, 3]); lstate = k.sb("lstate", [128, 8])
    cl = k.sb("cl", [128, 8]); cl2 = k.sb("cl2", [128, 8]); tmp8 = k.sb("tmp8", [128, 8])
    k.op("dve", lambda e: e.memset(lhist[:], 0.0), [], ["lhist"])
    k.op("dve", lambda e: e.memset(lstate[:], 0.0), [], ["lstate"])
    A_(k, tmp8[:], C["llam"][:], AF.Exp, ["c_llam"], ["tmp8"], scale=-1.0)
    A_(k, tmp8[:], tmp8[:], AF.Ln, ["tmp8"], ["tmp8"], bias=C["ones"][:, 0:1], scale=1.0)
    TS(k, "dve", cl[:], tmp8[:], -8.0, ALU.mult, ["tmp8"], ["cl"])
    TS(k, "dve", cl2[:], tmp8[:], -16.0, ALU.mult, ["tmp8"], ["cl2"])
    xbuf = Rot(k, "xbuf", 1, [128, 3 + TT])
    f32r = {n: Rot(k, n, (2 if n == "xc" else 1), [128, TT]) for n in ["xc", "gr", "gi", "ga", "gs"]}
    bfr = {n: Rot(k, n, (1 if n in ("qf", "kf", "vf") else 2), [128, TT], BF16) for n in ["xcb", "szb", "xmb", "qf", "kf", "vf", "yl"]}
    mhist = k.sb("mhist", [128, 16, 3]); k.op("dve", lambda e: e.memset(mhist[:], 0.0), [], ["mhist"])
    xmc = k.sb("xmc", [128, 8, TT], BF16)
    qT = k.sb("qT", [128, 8, TT], BF16); kT = k.sb("kT", [128, 8, TT], BF16)
    Ktm = k.sb("Ktm", [128, 4, 4, 256], BF16)
    Vext = k.sb("Vext", [128, 4, 4, 258], BF16)
    k.op("dve", lambda e: e.memset(Vext[:], 1.0), [], ["Vext"])
    Cst = k.sb("Cst", [128, 4, 2, 258]); Cbf = k.sb("Cbf", [128, 4, 2, 258], BF16)
    k.op("dve", lambda e: e.memset(Cst[:], 0.0), [], ["Cst"]); k.op("pool", lambda e: e.memset(Cbf[:], 0.0), [], ["Cbf"])
    G = k.sb("G", [128, TT]); k.op("dve", lambda e: e.memset(G[:], 0.0), [], ["G"])
    g_i = k.sb("g_i", [36, TT]); g_e = k.sb("g_e", [36, TT]); g_lf = k.sb("g_lf", [4, TT]); g_b = k.sb("g_b", [4, TT])
    g_amb = k.sb("g_amb", [4, TT]); g_cm = k.sb("g_cm", [4, TT]); g_mt = k.sb("g_mt", [4, TT]); g_bm = g_lf
    g_tmp = k.sb("g_tmp", [4, TT]); g_en = k.sb("g_en", [4, TT]); g_ew = g_cm; bmd = k.sb("bmd", [128, 4, 128]); k.op("dve", lambda e: e.memset(bmd[:], 0.0), [], ["bmd"])
    bmbr = Rot(k, "bmb", 1, [128, 512])
    mcar = k.sb("mcar", [4, 1]); k.op("dve", lambda e: e.memset(mcar[:], 0.0), [], ["mcar"])
    mnext = k.sb("mnext", [4, 4]); mprev = k.sb("mprev", [4, 4]); dl = k.sb("dl", [4, 4]); wlc = k.sb("wlc", [4, 4])
    dld = k.sb("dld", [128, 4, 4]); k.op("dve", lambda e: e.memset(dld[:], 0.0), [], ["dld"]); DEC = k.sb("DEC", [128, 16])
    GT = Rot(k, "GT", 2, [128, 128])
    Ghl = (Rot(k, "Ghi", 1, [128, 128], BF16), Rot(k, "Glo", 1, [128, 128], BF16))
    DTr = Rot(k, "DT", 1, [128, 128]); DTm = Rot(k, "DTm", 1, [128, 128]); PTr = Rot(k, "PT", 2, [128, 128], BF16)
    hnr = Rot(k, "hn", 2, [128, 258]); itr = Rot(k, "it", 1, [128, 258])
    ynr = Rot(k, "yn", 2, [128, 256], BF16); Kwr = Rot(k, "Kw", 2, [128, 256], BF16)
    smr = Rot(k, "sm", 2, [128, 16])
    ytm = Rot(k, "ytm", 2, [128, 128])

    def conv(src, hist_ap, cw, cb, e, out, outk, histk):
        buf, bk = src
        k.op("pool", lambda en: en.tensor_copy(out=buf[:, 0:3], in_=hist_ap), [histk], [bk])
        TS(k, "dve", out[:], buf[:, 3:3 + TT], cw[:, 4 * e + 3:4 * e + 4], ALU.mult, [bk], [outk], s2=cb[:, e:e + 1], op1=ALU.add)
        for j in range(3):
            STT(k, out[:], buf[:, j:j + TT], cw[:, 4 * e + j:4 * e + j + 1], ALU.mult, out[:], ALU.add, [bk, outk], [outk])
        k.op("pool", lambda en: en.tensor_copy(out=hist_ap, in_=buf[:, TT:TT + 3]), [bk], [histk])

    for ti in range(NT):
        t0 = ti * TT
        prenorm(k, C, xT_d, t0, hT, "g0pre", PSB)

        def inproj(e):
            wf, wfkey = wstf.get()
            k.dma(wf[:], win_d[e], W=[wfkey])
            w, wkey = wst.get()
            k.op("pool", lambda en: en.tensor_copy(out=w[:], in_=wf[:]), [wfkey], [wkey])
            ps, psk = PSA.get()
            for kc in range(16):
                MM(k, ps[:, :], w[:, kc * 128:(kc + 1) * 128], hT[:, kc, :], kc == 0, kc == 15, [wkey, "hT"], [psk])
            return ps, psk

        for c in range(8):
            ps, psk = inproj(c)
            xb = xbuf.get()
            A_(k, xb[0][:, 3:3 + TT], ps[:, :], AF.Copy, [psk], [xb[1]])
            xc, xck = f32r["xc"].get()
            conv(xb, lhist[:, c, :], C["lcw"], C["lcb"], c, xc, xck, "lhist")
            xcb, xcbk = bfr["xcb"].get()
            A_(k, xcb[:], xc[:], AF.Copy, [xck], [xcbk])
            pr, prk = PSB.get()
            MM(k, pr[:, :], lwa[:, c * 128:(c + 1) * 128], xcb[:], True, True, ["lwa", xcbk], [prk])
            pi, pik = PSC.get()
            MM(k, pi[:, :], lwx[:, c * 128:(c + 1) * 128], xcb[:], True, True, ["lwx", xcbk], [pik])
            gr, grk = f32r["gr"].get(); gi, gik = f32r["gi"].get(); ga, gak = f32r["ga"].get(); gs, gsk = f32r["gs"].get()
            A_(k, gr[:], pr[:, :], AF.Sigmoid, [prk], [grk], bias=C["lba"][:, c:c + 1], scale=1.0)
            A_(k, gi[:], pi[:, :], AF.Sigmoid, [pik], [gik], bias=C["lbx"][:, c:c + 1], scale=1.0)
            A_(k, ga[:], gr[:], AF.Exp, [grk, "cl"], [gak], scale=cl[:, c:c + 1])
            A_(k, gs[:], gr[:], AF.Exp, [grk, "cl2"], [gsk], scale=cl2[:, c:c + 1])
            A_(k, gs[:], gs[:], AF.Sqrt, [gsk], [gsk], bias=C["ones"][:, 0:1], scale=-1.0)
            gu, guk = gi, gik
            TTo(k, "pool", gu[:], gi[:], xc[:], ALU.mult, [gik, xck], [guk])
            TTo(k, "dve", gu[:], gu[:], gs[:], ALU.mult, [guk, gsk], [guk])
            gh, ghk = gr, grk
            k.op("dve", lambda e: e.tensor_tensor_scan(out=gh[:], data0=ga[:], data1=gu[:], initial=lstate[:, c:c + 1],
                                                      op0=ALU.mult, op1=ALU.add), [gak, guk, "lstate"], [ghk])
            k.op("pool", lambda e: e.tensor_copy(out=lstate[:, c:c + 1], in_=gh[:, TT - 1:TT]), [ghk], ["lstate"])
            pz, pzk = inproj(8 + c)
            szb, szk = bfr["szb"].get()
            A_(k, szb[:], pz[:, :], AF.Silu, [pzk], [szk])
            yl, ylk = bfr["yl"].get()
            TTo(k, "dve", yl[:], gh[:], szb[:], ALU.mult, [ghk, szk], [ylk])
            k.dma(y_d[:, YL + c, t0:t0 + TT], yl[:], R=[ylk], W=["y0T"], key="yout" + ylk)

        if STOP == 2:
            break
        own0 = None
        for j in range(16):
            own = j < 8
            ps, psk = inproj(16 + j)
            xb = xbuf.get()
            A_(k, xb[0][:, 3:3 + TT], ps[:, :], AF.Copy, [psk], [xb[1]])
            xmb, xmbk = bfr["xmb"].get()
            A_(k, xmb[:], ps[:, :], AF.Copy, [psk], [xmbk])
            xc, xck = f32r["xc"].get()
            conv(xb, mhist[:, j, :], C["mcw"], C["mcb"], j, xc, xck, "mhist")
            if own:
                xmcj, xmck = xmc[:, j, :], "xmc"
            else:
                tt_, xmck = bfr["xcb"].get(); xmcj = tt_[:]
            A_(k, xmcj, xc[:], AF.Silu, [xck], [xmck])
            outs = []
            for nm, wbd, src, srck in (("q", wq, xmcj, xmck), ("k", wk, xmcj, xmck), ("v", wv, xmb[:], xmbk)):
                pp, ppk = (PSB if nm != "k" else PSC).get()
                MM(k, pp[:, :], wbd[:, j * 128:(j + 1) * 128], src, True, True, [srck, "w" + nm], [ppk])
                if own and nm == "q":
                    dst, dk = qT[:, j, :], "qT"
                elif own and nm == "k":
                    dst, dk = kT[:, j, :], "kT"
                else:
                    tt_, dk = bfr[nm + "f"].get(); dst = tt_[:]
                if nm == "k":
                    k.op("dve", lambda e, dst=dst, pp=pp: e.tensor_copy(out=dst, in_=pp[:, :]), [ppk], [dk])
                else:
                    A_(k, dst, pp[:, :], AF.Copy, [ppk], [dk])
                outs.append((dst, dk))
            for qi, (dst, dk) in enumerate(outs):
                MM(k, PSG[:, :], wg[:, (qi * 16 + j) * 128:(qi * 16 + j + 1) * 128], dst, (j == 0 and qi == 0), (j == 15 and qi == 2),
                   ["wg", dk], ["psG"])
            if own:
                hh, half = j // 2, j % 2
                for tc in range(4):
                    pk_, pkk = PSB.get()
                    MM(k, pk_[:, 0:128], xmcj[:, tc * 128:(tc + 1) * 128], wk[:, j * 128:(j + 1) * 128], True, True, [xmck, "wk"], [pkk])
                    MM(k, pk_[:, 128:256], xmb[:, tc * 128:(tc + 1) * 128], wv[:, j * 128:(j + 1) * 128], True, True, [xmbk, "wv"], [pkk])
                    A_(k, Ktm[:, hh, tc, half * 128:(half + 1) * 128], pk_[:, 0:128], AF.Copy, [pkk], ["Ktm"])
                    k.op("dve", lambda e, pk_=pk_, tc=tc: e.tensor_copy(out=Vext[:, hh, tc, half * 128:(half + 1) * 128], in_=pk_[:, 128:256]), [pkk], ["Vext"])

        if STOP == 3:
            break
        gb = C["gbias"]
        A_(k, g_i[0:4, :], PSG[0:4, :], AF.Identity, ["psG"], ["g_i"], bias=gb[0:4, 0:1], scale=1.0)
        A_(k, g_e[32:36, :], PSG[32:36, :], AF.Sigmoid, ["psG"], ["g_e"], bias=gb[32:36, 0:1], scale=1.0)
        A_(k, g_e[32:36, :], g_e[32:36, :], AF.Ln, ["g_e"], ["g_e"])
        k.op("dve", lambda e: e.tensor_copy(out=g_lf[:], in_=g_e[32:36, :]), ["g_e"], ["g_lf"])
        k.op("dve", lambda e: e.tensor_tensor_scan(out=g_b[:], data0=C["rmask"][0:4, :], data1=g_lf[:], initial=0.0, op0=ALU.mult, op1=ALU.add),
             ["g_lf", "c_rmask"], ["g_b"])
        TTo(k, "dve", g_amb[:], g_i[0:4, :], g_b[:], ALU.subtract, ["g_i", "g_b"], ["g_amb"])
        k.op("dve", lambda e: e.tensor_tensor_scan(out=g_cm[:], data0=C["nmask"][0:4, :], data1=g_amb[:], initial=0.0, op0=ALU.add, op1=ALU.max),
             ["g_amb", "c_nmask"], ["g_cm"])
        k.op("dve", lambda e: e.tensor_tensor_scan(out=mnext[:], data0=g_cm[:, 127::128], data1=g_b[:, 127::128], initial=mcar[:, 0:1],
                                                  op0=ALU.max, op1=ALU.add), ["g_cm", "g_b", "mcar"], ["mnext"])
        k.op("dve", lambda e: e.tensor_copy(out=mprev[:, 0:1], in_=mcar[:, 0:1]), ["mcar"], ["mprev"])
        k.op("dve", lambda e: e.tensor_copy(out=mprev[:, 1:4], in_=mnext[:, 0:3]), ["mnext"], ["mprev"])
        k.op("dve", lambda e: e.tensor_copy(out=mcar[:, 0:1], in_=mnext[:, 3:4]), ["mnext", "mprev"], ["mcar"])
        TTo(k, "dve", wlc[:], g_b[:, 127::128], mnext[:], ALU.subtract, ["g_b", "mnext"], ["wlc"])
        TTo(k, "dve", dl[:], wlc[:], mprev[:], ALU.add, ["wlc", "mprev"], ["dl"])
        TS(k, "dve", wlc[:], wlc[:], -LN16, ALU.add, ["wlc"], ["wlc"])
        for c4 in range(4):
            sl = slice(c4 * 128, (c4 + 1) * 128)
            STT(k, g_mt[:, sl], g_cm[:, sl], mprev[:, c4:c4 + 1], ALU.max, g_b[:, sl], ALU.add, ["g_cm", "mprev", "g_b"], ["g_mt"])
        TTo(k, "dve", g_bm[:], g_b[:], g_mt[:], ALU.subtract, ["g_b", "g_mt"], ["g_lf"])
        TS(k, "dve", G[0:4, :], g_amb[:], -LN16, ALU.add, ["g_amb"], ["G"])
        for c4 in range(4):
            sl = slice(c4 * 128, (c4 + 1) * 128)
            A_(k, g_tmp[:, sl], g_bm[:, sl], AF.Exp, ["g_lf", "mprev"], ["g_tmp"], bias=mprev[:, c4:c4 + 1], scale=1.0)
        k.op("dve", lambda e: e.tensor_copy(out=G[32:36, :], in_=g_tmp[:]), ["g_tmp"], ["G"])
        A_(k, g_en[:], g_mt[:], AF.Exp, ["g_mt"], ["g_en"], scale=-1.0)
        k.op("dve", lambda e: e.tensor_copy(out=G[64:68, :], in_=g_en[:]), ["g_en"], ["G"])
        for c4 in range(4):
            sl = slice(c4 * 128, (c4 + 1) * 128)
            A_(k, g_ew[:, sl], g_amb[:, sl], AF.Exp, ["g_amb", "wlc"], ["g_cm"], bias=wlc[:, c4:c4 + 1], scale=1.0)
        k.op("dve", lambda e: e.tensor_copy(out=G[96:100, :], in_=g_ew[:]), ["g_cm"], ["G"])
        for hh in range(4):
            TS(k, "dve", dld[0:4, hh, :], dl[:], C["eye4"][:, hh:hh + 1], ALU.mult, ["dl"], ["dld"])
        pd, pdk = PSC.get()
        MM32(k, C, pd[:, 0:16], "ones4", dld[:].rearrange("p a b -> p (a b)"), True, True, ["dld"], [pdk])
        A_(k, DEC[:], pd[:, 0:16], AF.Exp, [pdk], ["DEC"])

        if STOP == 4:
            break
        for c4 in range(4):
            sl = slice(c4 * 128, (c4 + 1) * 128)
            gt, gtk = GT.get()
            pg, pgk = PSC.get()
            if not _BF():
                k.op("pe", lambda e: e.transpose(pg[:, 0:128], G[:, sl], C["ident"][:, :]), ["G", "c_ident"], [pgk])
                A_(k, gt[:], pg[:, 0:128], AF.Copy, [pgk], [gtk])
            else:
                ghi, ghik = Ghl[0].get(); glo, glok = Ghl[1].get()
                k.op("dve", lambda e: e.tensor_copy(out=ghi[:], in_=G[:, sl]), ["G"], [ghik])
                TTo(k, "dve", glo[:], G[:, sl], ghi[:], ALU.subtract, ["G", ghik], [glok])
                ptg, ptgk = PST.get()
                k.op("pe", lambda e: e.transpose(ptg[:, 0:128], ghi[:], identb[:, :]), [ghik, "identb"], [ptgk])
                k.op("pe", lambda e: e.transpose(ptg[:, 128:256], glo[:], identb[:, :]), [glok, "identb"], [ptgk])
                A_(k, gt[:], ptg[:, 0:128], AF.Copy, [ptgk], [gtk])
                TTo(k, "dve", gt[:], gt[:], ptg[:, 128:256], ALU.add, [gtk, ptgk], [gtk])
            for hh in range(4):
                TS(k, "dve", bmd[0:4, hh, :], g_bm[:, sl], C["eye4"][:, hh:hh + 1], ALU.mult, ["g_lf"], ["bmd"])
            pbm_, pbmk_ = PSB.get()
            MM32(k, C, pbm_[:, :], "ones4", bmd[:].rearrange("p a b -> p (a b)"), True, True, ["bmd"], [pbmk_])
            pbm, pbmk = bmbr.get()
            A_(k, pbm[:], pbm_[:, :], AF.Copy, [pbmk_], [pbmk])
            for hh in range(4):
                psc, psck = PSC.get()
                for dc in range(2):
                    MM(k, psc[:, 0:128], kT[:, 2 * hh + dc, sl], qT[:, 2 * hh + dc, sl], dc == 0, dc == 1, ["kT", "qT"], [psck])
                dt_, dtk = DTr.get()
                A_(k, dt_[:], pbm[:, hh * 128:(hh + 1) * 128], AF.Exp, [pbmk, gtk], [dtk], bias=gt[:, hh:hh + 1], scale=1.0)
                dm, dmk = DTm.get()
                TTo(k, "pool", dm[:], dt_[:], C["maskle"][:], ALU.mult, [dtk, "c_maskle"], [dmk])
                pt, ptk = PTr.get()
                TTo(k, "dve", pt[:], psc[:, 0:128], dm[:], ALU.mult, [psck, dmk], [ptk])
                pn, pnk = PSA.get()
                MM(k, pn[:, 0:257], pt[:], Vext[:, hh, c4, 0:257], True, True, [ptk, "Vext"], [pnk])
                pi_, pik = PSB.get()
                for dc in range(2):
                    MM(k, pi_[:, 0:257], qT[:, 2 * hh + dc, sl], Cbf[:, hh, dc, 0:257], dc == 0, dc == 1, ["qT", "Cbf"], [pik])
                it, itk = itr.get()
                A_(k, it[:, 0:257], pi_[:, 0:257], AF.Copy, [pik, gtk], [itk], scale=gt[:, 32 + hh:33 + hh])
                hn, hnk = hnr.get()
                TTo(k, "dve", hn[:, 0:257], pn[:, 0:257], it[:, 0:257], ALU.add, [pnk, itk], [hnk])
                sm, smk = smr.get()
                TS(k, "dve", sm[:, 0:1], hn[:, 256:257], -1.0, ALU.mult, [hnk], [smk])
                STT(k, sm[:, 1:2], hn[:, 256:257], gt[:, 64 + hh:65 + hh], ALU.max, sm[:, 0:1], ALU.max, [hnk, gtk, smk], [smk])
                k.op("dve", lambda e: e.bn_stats(out=sm[:, 2:8], in_=hn[:, 0:256]), [hnk], [smk])
                k.op("dve", lambda e: e.bn_aggr(out=sm[:, 8:10], in_=sm[:, 2:8]), [smk], [smk])
                STT(k, sm[:, 10:11], sm[:, 1:2], 1e-6, ALU.mult, sm[:, 1:2], ALU.mult, [smk], [smk])
                A_(k, sm[:, 11:12], sm[:, 9:10], AF.Sqrt, [smk], [smk], bias=sm[:, 10:11], scale=1.0)
                k.op("dve", lambda e: e.reciprocal(out=sm[:, 12:13], in_=sm[:, 11:12]), [smk], [smk])
                yn, ynk = ynr.get()
                TS(k, "dve", yn[:], hn[:, 0:256], sm[:, 8:9], ALU.subtract, [hnk, smk], [ynk], s2=sm[:, 12:13], op1=ALU.mult)
                ptt, pttk = PST.get()
                for dc in range(2):
                    k.op("pe", lambda e, dc=dc: e.transpose(ptt[:, dc * 128:(dc + 1) * 128], yn[:, dc * 128:(dc + 1) * 128], identb[:, :]), [ynk, "identb"], [pttk])
                for dc in range(2):
                    j = 2 * hh + dc
                    yt, ytk = ytm.get()
                    TS(k, "pool", yt[:], xmc[:, j, sl], C["mskip"][:, j:j + 1], ALU.mult, ["xmc"], [ytk])
                    STT(k, yt[:], ptt[:, dc * 128:(dc + 1) * 128], C["mgn"][:, j:j + 1], ALU.mult, yt[:], ALU.add, [pttk, ytk], [ytk])
                    k.op("pool", lambda e, yt=yt, j=j: e.tensor_copy(out=yT[:, j, sl], in_=yt[:]), [ytk], ["yT"])
                kw, kwk = Kwr.get()
                TS(k, "pool", kw[:], Ktm[:, hh, c4, :], gt[:, 96 + hh:97 + hh], ALU.mult, ["Ktm", gtk], [kwk])
                for dc in range(2):
                    pcn, pcnk = PSC.get()
                    MM(k, pcn[:, 0:257], kw[:, dc * 128:(dc + 1) * 128], Vext[:, hh, c4, 0:257], True, True, [kwk, "Vext"], [pcnk])
                    STT(k, Cst[:, hh, dc, 0:257], Cst[:, hh, dc, 0:257], DEC[:, hh * 4 + c4:hh * 4 + c4 + 1], ALU.mult, pcn[:, 0:257], ALU.add,
                        [pcnk, "DEC", "Cst"], ["Cst"])
                k.op("pool", lambda e: e.tensor_copy(out=Cbf[:, hh, :, :], in_=Cst[:, hh, :, :]), ["Cst"], ["Cbf"])
        for j in range(8):
            pz, pzk = inproj(32 + j)
            szb, szk = bfr["szb"].get()
            A_(k, szb[:], pz[:, :], AF.Silu, [pzk], [szk])
            TTo(k, "dve", yT[:, j, :], yT[:, j, :], szb[:], ALU.mult, ["yT", szk], ["yT"])
        for j in range(8):
            k.dma(y_d[:, YM + j, t0:t0 + TT], yT[:, j, :], R=["yT"], W=["y0T"], key="youtm")
    if yoff == (0, 8):
        k.finish(["y0T"])


TB = 512
NTB = 4


def build_B(k, yT_d=None, xT_d=None, o_d=None, ntb=None, ykey="yTin", okey="xoT", final=True, xkey="xres"):
    ntb = NTB if ntb is None else ntb
    if yT_d is None:
        yT_d = k.dram("yT", [128, 32, 2048], BF16)
    if xT_d is None:
        xT_d = k.dram("xT", [128, 16, 2048])
    wout_d = k.dram("wout", [16, 128, 32 * 128])
    if o_d is None:
        o_d = k.dram("xoT", [128, 16, 2048], kind="ExternalOutput")
    C = load_consts(k, [("ones", [128, 128]), ("eps6", [128, 1]), ("gpost", [128, 16])])
    yt = k.sb("yt", [128, 32, TB], BF16)
    wstf = Rot(k, "wstf", 2, [128, 32 * 128]); wst = Rot(k, "wst", 2, [128, 32 * 128], BF16)
    oT = k.sb("oT", [128, 16, TB])
    sqr = Rot(k, "sq", 2, [128, TB]); rstd = k.sb("rstd", [128, TB])
    xr = Rot(k, "xt", 3, [128, TB]); outr = Rot(k, "xo", 3, [128, TB])
    PSA = Rot(k, "psA", 3, [128, 512], ps=True); PSN = k.ps("psN", [128, 512])
    for ti in range(ntb):
        t0 = ti * TB
        for c in range(32):
            k.dma(yt[:, c, :], yT_d[:, c, t0:t0 + TB], R=[ykey], W=["yt"], key="ytl")
        for dc in range(16):
            wf, wfk = wstf.get()
            k.dma(wf[:], wout_d[dc], W=[wfk])
            w, wk = wst.get()
            k.op("pool", lambda e: e.tensor_copy(out=w[:], in_=wf[:]), [wfk], [wk])
            ps, psk = PSA.get()
            for kc in range(32):
                MM(k, ps[:, :], w[:, kc * 128:(kc + 1) * 128], yt[:, kc, :], kc == 0, kc == 31, [wk, "yt"], [psk])
            A_(k, oT[:, dc, :], ps[:, :], AF.Copy, [psk], ["oT"])
            sq, sqk = sqr.get()
            k.op("pool", lambda e: e.tensor_tensor(out=sq[:], in0=oT[:, dc, :], in1=oT[:, dc, :], op=ALU.mult), ["oT"], [sqk])
            MM32(k, C, PSN[:, :], "ones", sq[:], dc == 0, dc == 15, [sqk], ["psN"], precise=False)
        A_(k, rstd[:], PSN[:, :], AF.Ln, ["psN"], ["rstd"], bias=C["eps6"][:, 0:1], scale=1.0 / 2048.0)
        A_(k, rstd[:], rstd[:], AF.Exp, ["rstd"], ["rstd"], scale=-0.5)
        for dc in range(16):
            xt, xk = xr.get()
            k.dma(xt[:], xT_d[:, dc, t0:t0 + TB], R=[xkey], W=[xk])
            xo, xok = outr.get()
            STT(k, xo[:], oT[:, dc, :], C["gpost"][:, dc:dc + 1], ALU.mult, rstd[:], ALU.mult, ["oT", "rstd", "c_gpost"], [xok])
            TTo(k, "dve", xo[:], xo[:], xt[:], ALU.add, [xok, xk], [xok])
            k.dma(o_d[:, dc, t0:t0 + TB], xo[:], R=[xok], W=[okey], key="o" + xok)
    if final:
        k.finish([okey])


TT = 512
NT = 8
S = 4096
NEG_E05 = -float(np.exp(-0.5))


def CP(k, eng, out, in_, R, W):
    if eng == "act":
        return A_(k, out, in_, AF.Copy, R, W)
    return k.op(eng, lambda e: e.tensor_copy(out=out, in_=in_), R, W)


def TR(k, out, in_, ident, R, W):
    return k.op("pe", lambda e: e.transpose(out, in_, ident), R, W)


def build_C(k, xT_d=None, y_d=None, yoff=None):
    if xT_d is None:
        xT_d = k.dram("xT", [128, 16, S])
    win_d = k.dram("win", [64, 128, 16 * 128])
    if y_d is None:
        y_d = k.dram("y1T", [128, 16, S], BF16, kind="ExternalOutput")
        yoff = 0
        fused = False
    else:
        fused = True
    C = load_consts(k, [("ones", [128, 128]), ("ident", [128, 128]), ("bo64", [128, 128]), ("mask2", [128, 256]), ("maskN", [128, 128]),
                        ("rmask", [128, TT]), ("hmask", [128, 2]), ("eps6", [128, 1]), ("epsgn", [128, 1]), ("tiny", [128, 1]),
                        ("g1pre", [128, 16]), ("murkv", [128, 48]), ("muw", [128, 16]), ("mua", [128, 16]),
                        ("w0", [128, 16]), ("a0", [128, 16]), ("kk_", [128, 16]), ("ka", [128, 16]), ("rk", [128, 16]),
                        ("gng", [128, 16]), ("gnb", [128, 16])])
    w1_d = k.dram("w1", [128, 16 * 96]); a1_d = k.dram("a1", [128, 16 * 96])
    w2_d = k.dram("w2", [96, 2048]); a2_d = k.dram("a2", [96, 2048])
    w1b = k.sb("w1b", [128, 16 * 96], BF16); a1b = k.sb("a1b", [128, 16 * 96], BF16)
    w2b = k.sb("w2b", [96, 2048], BF16); a2b = k.sb("a2b", [96, 2048], BF16)
    wstf = Rot(k, "wstf", 2, [128, 2048]); wst = Rot(k, "wst", 2, [128, 2048], BF16)
    for t, d, nm, rows, n in [(w1b, w1_d, "w1b", 128, 1536), (a1b, a1_d, "a1b", 128, 1536), (w2b, w2_d, "w2b", 96, 2048), (a2b, a2_d, "a2b", 96, 2048)]:
        wf, wfk = wstf.get()
        k.dma(wf[0:rows, 0:n], d, W=[wfk])
        CP(k, "pool", t[:], wf[0:rows, 0:n], [wfk], [nm])
    identb = k.sb("identb", [128, 128], BF16); CP(k, "dve", identb[:], C["ident"][:], ["c_ident"], ["identb"])
    mask2 = C["mask2"]; maskN = C["maskN"]

    hT = k.sb("hT", [128, 16, TT + 1], BF16)
    k.op("dve", lambda e: e.memset(hT[:], 0.0), [], ["hT"])
    xr = Rot(k, "xt", 2, [128, TT]); sqr = Rot(k, "sq", 2, [128, TT]); rstd = k.sb("rstd", [128, TT])
    PSA = Rot(k, "psA", 2, [128, 512], ps=True)
    PSG = Rot(k, "psG", 4, [128, 512], ps=True)
    PST = k.ps("psT", [128, 6, 128], BF16)
    PSL = k.ps("psL", [128, 512])
    tanhT = k.sb("tanhT", [96, TT], BF16); axT = k.sb("axT", [96, TT], BF16)
    ST = k.sb("ST", [128, 16, 64]); k.op("dve", lambda e: e.memset(ST[:], 0.0), [], ["ST"])
    hist = k.sb("hist", [128, 48]); k.op("dve", lambda e: e.memset(hist[:], 0.0), [], ["hist"])
    F = {n: Rot(k, n, (1 if n in ("cm", "cpm", "Pm", "Pp", "Pi", "cum", "cpv", "kkr", "sq2", "t1") else 2), [128, TT]) for n in ["dh", "rs_", "ks_", "vs_", "lw", "as_", "kkr", "sq2", "kk", "k2", "bv", "cum", "cpv", "cm", "cpm", "Pm", "Pp", "Pi", "t1"]}
    buf = Rot(k, "rbuf", 3, [128, TT + 1])
    xwr = Rot(k, "xw", 2, [128, TT], BF16); xar = Rot(k, "xa", 2, [128, TT], BF16)
    AR = [k.sb(f"AR{i}", [128, 4, 2, 128], BF16) for i in range(2)]
    BT = [k.sb(f"BT{i}", [128, TT], BF16) for i in range(2)]
    KT = [k.sb(f"KT{i}", [128, TT], BF16) for i in range(2)]
    vb = [k.sb(f"vb{i}", [128, TT], BF16) for i in range(2)]
    szb = [k.sb(f"szb{i}", [128, TT], BF16) for i in range(2)]
    bon = [k.sb(f"bon{i}", [128, TT]) for i in range(2)]
    Yfm = [k.sb(f"Yfm{i}", [128, TT]) for i in range(2)]
    sc = [k.sb(f"sc{i}", [128, 5, 4]) for i in range(2)]
    NRm = k.sb("NRm", [128, 4, 256], BF16); MKm = k.sb("MKm", [128, 4, 256], BF16)
    X = [k.sb(f"X{i}", [128, 4, 128], BF16) for i in range(2)]; XT = [k.sb(f"XT{i}", [128, 4, 128], BF16) for i in range(2)]
    Wc32 = k.sb("Wc32", [128, 512]); Wcb = [k.sb(f"Wcb{i}", [128, 512], BF16) for i in range(2)]
    BTz = [[k.sb(f"BTz{i}{h}", [128, TT], BF16) for h in range(2)] for i in range(2)]
    KTz = [[k.sb(f"KTz{i}{h}", [128, TT], BF16) for h in range(2)] for i in range(2)]
    Az = [[k.sb(f"Az{i}{h}", [128, 4, 128], BF16) for h in range(2)] for i in range(2)]
    S0bd = [k.sb(f"S0bd{i}", [128, 128], BF16) for i in range(2)]
    Vp = [[k.sb(f"Vp{i}{h}", [128, 128], BF16) for h in range(2)] for i in range(2)]
    U2 = [k.sb(f"U2{i}", [128, 128], BF16) for i in range(2)]
    for i in range(2):
        k.op("dve", lambda e: e.memset(S0bd[i][:], 0.0), [], [f"S0bd{i}"])
        for h in range(2):
            k.op("dve", lambda e: e.memset(Vp[i][h][:], 0.0), [], [f"Vp{i}{h}"])
    VBK = k.sb("VBK", [128, 6, 128], BF16); bkc = Rot(k, "bkc", 2, [128, 128], BF16)
    S0m = [k.sb(f"S0m{i}", [128, 64], BF16) for i in range(2)]
    yo = Rot(k, "yo", 2, [128, TT], BF16)
    hn = {n: Rot(k, n, 1, [128, TT]) for n in ["ysq", "mean", "msq", "var", "yc"]}

    for ti in range(NT):
        t0 = ti * TT
        ps, psk = PSG.get()
        for c in range(16):
            xt, xk = xr.get()
            k.dma(xt[:], xT_d[:, c, t0:t0 + TT], R=["x1T"], W=[xk])
            sq, sqk = sqr.get()
            A_(k, sq[:], xt[:], AF.Square, [xk], [sqk])
            MM32(k, C, ps[:, :], "ones", sq[:], c == 0, c == 15, [sqk], [psk], precise=False)
        A_(k, rstd[:], ps[:, :], AF.Ln, [psk], ["rstd"], bias=C["eps6"][:, 0:1], scale=1.0 / 2048.0)
        A_(k, rstd[:], rstd[:], AF.Exp, ["rstd"], ["rstd"], scale=-0.5)
        for c in range(16):
            xt, xk = xr.get()
            k.dma(xt[:], xT_d[:, c, t0:t0 + TT], R=["x1T"], W=[xk])
            STT(k, hT[:, c, 1:TT + 1], xt[:], C["g1pre"][:, c:c + 1], ALU.mult, rstd[:], ALU.mult, [xk, "rstd"], ["hT"])
        pa, pak = PSG.get()
        for c in range(16):
            dh, dhk = F["dh"].get()
            TTo(k, "dve", dh[:], hT[:, c, 0:TT], hT[:, c, 1:TT + 1], ALU.subtract, ["hT"], [dhk])
            xw, xwk = xwr.get(); xa, xak = xar.get()
            STT(k, xw[:], dh[:], C["muw"][:, c:c + 1], ALU.mult, hT[:, c, 1:TT + 1], ALU.add, [dhk, "hT"], [xwk])
            STT(k, xa[:], dh[:], C["mua"][:, c:c + 1], ALU.mult, hT[:, c, 1:TT + 1], ALU.add, [dhk, "hT"], [xak])
            MM(k, PSL[0:96, :], w1b[:, c * 96:(c + 1) * 96], xw[:], c == 0, c == 15, ["w1b", xwk], ["psL"])
            MM(k, pa[0:96, :], a1b[:, c * 96:(c + 1) * 96], xa[:], c == 0, c == 15, ["a1b", xak], [pak])
        A_(k, tanhT[:], PSL[0:96, :], AF.Tanh, ["psL"], ["tanhT"])
        A_(k, axT[:], pa[0:96, :], AF.Copy, [pak], ["axT"])

        def inproj(e):
            wf, wfk = wstf.get()
            k.dma(wf[:], win_d[e], W=[wfk])
            w, wkey = wst.get()
            CP(k, "pool", w[:], wf[:], [wfk], [wkey])
            ps, psk = PSA.get()
            for kc in range(16):
                MM(k, ps[:, :], w[:, kc * 128:(kc + 1) * 128], hT[:, kc, 1:TT + 1], kc == 0, kc == 15, [wkey, "hT"], [psk])
            return ps, psk

        for p in range(8):
            for fi in range(2):
                fc = 2 * p + fi
                cs = slice(fc, fc + 1)
                vals = []
                for q, nm in enumerate(["rs_", "ks_", "vs_"]):
                    ps, psk = inproj(p * 8 + fi * 4 + q)
                    b_, bk_ = buf.get()
                    A_(k, b_[:, 1:TT + 1], ps[:, :], AF.Copy, [psk], [bk_])
                    CP(k, "pool", b_[:, 0:1], hist[:, fc * 3 + q:fc * 3 + q + 1], ["hist"], [bk_])
                    d_, dk_ = F["dh"].get()
                    TTo(k, "dve", d_[:], b_[:, 0:TT], b_[:, 1:TT + 1], ALU.subtract, [bk_], [dk_])
                    o_, ok_ = F[nm].get()
                    STT(k, o_[:], d_[:], C["murkv"][:, fc * 3 + q:fc * 3 + q + 1], ALU.mult, b_[:, 1:TT + 1], ALU.add, [dk_, bk_], [ok_])
                    CP(k, "pool", hist[:, fc * 3 + q:fc * 3 + q + 1], b_[:, TT:TT + 1], [bk_], ["hist"])
                    vals.append((o_, ok_))
                (r_s, rk_), (k_s, kk_k), (v_s, vk_) = vals
                pz, pzk = inproj(p * 8 + fi * 4 + 3)
                A_(k, szb[fi][:], pz[:, :], AF.Silu, [pzk], [f"szb{fi}"])
                pl, plk = PSG.get()
                MM(k, pl[:, :], w2b[:, fc * 128:(fc + 1) * 128], tanhT[:], True, True, ["w2b", "tanhT"], [plk])
                lw, lwk = F["lw"].get()
                A_(k, lw[:], pl[:, :], AF.Sigmoid, [plk], [lwk], bias=C["w0"][:, cs], scale=1.0)
                TS(k, "pool", lw[:], lw[:], NEG_E05, ALU.mult, [lwk], [lwk])
                pl2, pl2k = PSG.get()
                MM(k, pl2[:, :], a2b[:, fc * 128:(fc + 1) * 128], axT[:], True, True, ["a2b", "axT"], [pl2k])
                a_s, ak_ = F["as_"].get()
                A_(k, a_s[:], pl2[:, :], AF.Sigmoid, [pl2k], [ak_], bias=C["a0"][:, cs], scale=1.0)
                kkr, kkrk = F["kkr"].get()
                TS(k, "pool", kkr[:], k_s[:], C["kk_"][:, cs], ALU.mult, [kk_k], [kkrk])
                sq2, sq2k = F["sq2"].get()
                A_(k, sq2[:], kkr[:], AF.Square, [kkrk], [sq2k])
                pss, pssk = PSG.get()
                MM32(k, C, pss[:, :], "bo64", sq2[:], True, True, [sq2k], [pssk], precise=False)
                A_(k, sq2[:], pss[:, :], AF.Ln, [pssk], [sq2k], bias=C["tiny"][:, 0:1], scale=1.0)
                A_(k, sq2[:], sq2[:], AF.Exp, [sq2k], [sq2k], scale=-0.5)
                kk, kkk = F["kk"].get()
                TTo(k, "dve", kk[:], kkr[:], sq2[:], ALU.mult, [kkrk, sq2k], [kkk])
                t1, t1k = F["t1"].get()
                TS(k, "pool", t1[:], a_s[:], -1.0, ALU.add, [ak_], [t1k], s2=C["ka"][:, cs], op1=ALU.mult)
                k2, k2k = F["k2"].get()
                STT(k, k2[:], t1[:], 1.0, ALU.add, k_s[:], ALU.mult, [t1k, kk_k], [k2k])
                bv, bvk = F["bv"].get()
                TTo(k, "pool", bv[:], kk[:], a_s[:], ALU.mult, [kkk, ak_], [bvk])
                STT(k, t1[:], r_s[:], C["rk"][:, cs], ALU.mult, k2[:], ALU.mult, [rk_, k2k], [t1k])
                pb_, pbk = PSG.get()
                MM32(k, C, pb_[:, :], "bo64", t1[:], True, True, [t1k], [pbk], precise=False)
                TTo(k, "dve", bon[fi][:], pb_[:, :], v_s[:], ALU.mult, [pbk, vk_], [f"bon{fi}"])
                A_(k, vb[fi][:], v_s[:], AF.Copy, [vk_], [f"vb{fi}"])
                cum, cumk = F["cum"].get()
                k.op("dve", lambda e: e.tensor_tensor_scan(out=cum[:], data0=C["rmask"][:, :], data1=lw[:], initial=0.0, op0=ALU.mult, op1=ALU.add),
                     [lwk, "c_rmask"], [cumk])
                cpv, cpvk = F["cpv"].get()
                TTo(k, "pool", cpv[:], cum[:], lw[:], ALU.subtract, [cumk, lwk], [cpvk])
                s_ = sc[fi]; sk_ = f"sc{fi}"
                CP(k, "dve", s_[:, 0, :], cum[:, 63::128], [cumk], [sk_])
                cm, cmk = F["cm"].get(); cpm, cpmk = F["cpm"].get()
                for c4 in range(4):
                    sl = slice(c4 * 128, (c4 + 1) * 128)
                    TS(k, "dve", cm[:, sl], cum[:, sl], s_[:, 0, c4:c4 + 1], ALU.subtract, [cumk, sk_], [cmk])
                    TS(k, "pool", cpm[:, sl], cpv[:, sl], s_[:, 0, c4:c4 + 1], ALU.subtract, [cpvk, sk_], [cpmk])
                Pm, Pmk = F["Pm"].get(); Pp, Ppk = F["Pp"].get(); Pi, Pik = F["Pi"].get()
                A_(k, Pm[:], cm[:], AF.Exp, [cmk], [Pmk])
                A_(k, Pp[:], cpm[:], AF.Exp, [cpmk], [Ppk])
                A_(k, Pi[:], cm[:], AF.Exp, [cmk], [Pik], scale=-1.0)
                A_(k, s_[:, 2, :], s_[:, 0, :], AF.Exp, [sk_], [sk_])
                A_(k, s_[:, 3, :], cm[:, 127::128], AF.Exp, [cmk, sk_], [sk_])
                A_(k, s_[:, 4, :], cum[:, 127::128], AF.Exp, [cumk, sk_], [sk_])
                ar = AR[fi]; ark = f"AR{fi}"
                STT(k, ar[:, :, 0, :], kk[:].rearrange("p (c t) -> p c t", c=4), -1.0, ALU.mult, Pp[:].rearrange("p (c t) -> p c t", c=4), ALU.mult,
                    [kkk, Ppk], [ark])
                TTo(k, "pool", ar[:, :, 1, :], r_s[:].rearrange("p (c t) -> p c t", c=4), Pm[:].rearrange("p (c t) -> p c t", c=4), ALU.mult,
                    [rk_, Pmk], [ark])
                TTo(k, "dve", BT[fi][:], bv[:], Pi[:], ALU.mult, [bvk, Pik], [f"BT{fi}"])
                TTo(k, "pool", KT[fi][:], k2[:], Pi[:], ALU.mult, [k2k, Pik], [f"KT{fi}"])

            heads = [(fi, hh) for fi in range(2) for hh in range(2)]
            for fi in range(2):
                for hh in range(2):
                    hm = C["hmask"][:, hh:hh + 1]
                    TS(k, "dve", BTz[fi][hh][:], BT[fi][:], hm, ALU.mult, [f"BT{fi}"], [f"BTz{fi}{hh}"])
                    TS(k, "dve", KTz[fi][hh][:], KT[fi][:], hm, ALU.mult, [f"KT{fi}"], [f"KTz{fi}{hh}"])
                    TS(k, "dve", Az[fi][hh][:], AR[fi][:, :, 0, :], hm, ALU.mult, [f"AR{fi}"], [f"Az{fi}{hh}"])
            for c4 in range(4):
                sl = slice(c4 * 128, (c4 + 1) * 128)
                for fi in range(2):
                    fc = 2 * p + fi
                    TR(k, PST[:, fi * 3 + 0, :], vb[fi][:, sl], identb[:, :], [f"vb{fi}", "identb"], ["psT"])
                    for j, src in ((1, BT[fi]), (2, KT[fi])):
                        t_, tk_ = bkc.get()
                        TS(k, "pool", t_[:], src[:, sl], sc[fi][:, 3, c4:c4 + 1], ALU.mult, [f"BT{fi}", f"KT{fi}", f"sc{fi}"], [tk_])
                        TR(k, PST[:, fi * 3 + j, :], t_[:], identb[:, :], [tk_, "identb"], ["psT"])
                    for hh in range(2):
                        ps_ = slice(hh * 64, (hh + 1) * 64)
                        TS(k, "dve", S0bd[fi][ps_, ps_], ST[ps_, fc, :], sc[fi][ps_, 2, c4:c4 + 1], ALU.mult, ["ST", f"sc{fi}"], [f"S0bd{fi}"])
                A_(k, VBK[:], PST[:], AF.Copy, ["psT"], ["VBK"])
                for fi in range(2):
                    for hh in range(2):
                        cs_ = slice(hh * 64, (hh + 1) * 64)
                        CP(k, "dve", Vp[fi][hh][:, cs_], VBK[:, fi * 3, cs_], ["VBK"], [f"Vp{fi}{hh}"])
                for dstm, srcz, dk in ((NRm, BTz, "NRm"), (MKm, KTz, "MKm")):
                    for half in range(2):
                        pn, pnk = PSG.get()
                        for j in range(2):
                            fi, hh = heads[half * 2 + j]
                            MM(k, pn[:, j * 256:(j + 1) * 256], srcz[fi][hh][:, sl], AR[fi][:, c4, :, :], True, True,
                               [f"BTz{fi}{hh}", f"KTz{fi}{hh}", f"AR{fi}"], [pnk])
                        for j in range(2):
                            TTo(k, "dve", dstm[:, half * 2 + j, :], pn[:, j * 256:(j + 1) * 256], mask2[:, :], ALU.mult, [pnk], [dk])
                pn, pnk = PSG.get()
                for h4, (fi, hh) in enumerate(heads):
                    MM(k, pn[:, h4 * 128:(h4 + 1) * 128], Az[fi][hh][:, c4, :], BT[fi][:, sl], True, True, [f"Az{fi}{hh}", f"BT{fi}"], [pnk])
                for h4 in range(4):
                    TTo(k, "dve", X[0][:, h4, :], pn[:, h4 * 128:(h4 + 1) * 128], maskN[:, :], ALU.mult, [pnk], ["X0"])
                CP(k, "pool", XT[0][:, :, :], NRm[:, :, 0:128], ["NRm"], ["XT0"])
                pw, pwk = PSG.get()
                for h4, (fi, hh) in enumerate(heads):
                    o = pw[:, h4 * 128:(h4 + 1) * 128]
                    MM(k, o, Az[fi][hh][:, c4, :], S0bd[fi][:, :], True, False, [f"Az{fi}{hh}", f"S0bd{fi}"], [pwk])
                    MM(k, o, MKm[:, h4, 0:128], Vp[fi][hh][:, :], False, True, ["MKm", f"Vp{fi}{hh}"], [pwk])
                A_(k, Wc32[:], pw[:, :], AF.Copy, [pwk], ["Wc32"])
                CP(k, "dve", Wcb[0][:], pw[:, :], [pwk], ["Wcb0"])
                cur = 0
                for i in range(7):
                    xi, xti = X[i % 2], XT[i % 2]; xik, xtik = f"X{i % 2}", f"XT{i % 2}"
                    pxw, pxwk = PSG.get()
                    for h4 in range(4):
                        MM(k, pxw[:, h4 * 128:(h4 + 1) * 128], xti[:, h4, :], Wcb[cur][:, h4 * 128:(h4 + 1) * 128], True, True, [xtik, f"Wcb{cur}"], [pxwk])
                    if i < 6:
                        px, pxk = PSG.get(); pxt, pxtk = PSG.get()
                        for h4 in range(4):
                            MM(k, px[:, h4 * 128:(h4 + 1) * 128], xti[:, h4, :], xi[:, h4, :], True, True, [xtik, xik], [pxk])
                        for h4 in range(4):
                            MM(k, pxt[:, h4 * 128:(h4 + 1) * 128], xi[:, h4, :], xti[:, h4, :], True, True, [xtik, xik], [pxtk])
                    TTo(k, "dve", Wc32[:], pxw[:, :], Wc32[:], ALU.add, [pxwk, "Wc32"], ["Wc32"])
                    cur = 1 - cur
                    CP(k, "pool", Wcb[cur][:], Wc32[:], ["Wc32"], [f"Wcb{cur}"])
                    if i < 6:
                        nx, nxt = X[(i + 1) % 2], XT[(i + 1) % 2]
                        A_(k, nx[:].rearrange("p h f -> p (h f)"), px[:, :], AF.Copy, [pxk], [f"X{(i + 1) % 2}"])
                        CP(k, "dve", nxt[:].rearrange("p h f -> p (h f)"), pxt[:, :], [pxtk], [f"XT{(i + 1) % 2}"])
                U = Wcb[cur]; Uk = f"Wcb{cur}"
                py, pyk = PSG.get()
                for fi in range(2):
                    o = py[:, fi * 128:(fi + 1) * 128]
                    MM(k, o, S0bd[fi][:, :], AR[fi][:, c4, 1, :], True, False, [f"S0bd{fi}", f"AR{fi}"], [pyk])
                    for hh in range(2):
                        h4 = 2 * fi + hh
                        MM(k, o, U[:, h4 * 128:(h4 + 1) * 128], NRm[:, h4, 128:256], False, False, [Uk, "NRm"], [pyk])
                        MM(k, o, Vp[fi][hh][:, :], MKm[:, h4, 128:256], False, hh == 1, [f"Vp{fi}{hh}", "MKm"], [pyk])
                for fi in range(2):
                    A_(k, Yfm[fi][:, sl], py[:, fi * 128:(fi + 1) * 128], AF.Copy, [pyk], [f"Yfm{fi}"])
                pst, pstk = PSG.get()
                for fi in range(2):
                    TTo(k, "dve", U2[fi][:], U[:, (2 * fi) * 128:(2 * fi + 1) * 128], U[:, (2 * fi + 1) * 128:(2 * fi + 2) * 128], ALU.add, [Uk], [f"U2{fi}"])
                    o = pst[:, fi * 128:(fi + 1) * 128]
                    MM(k, o, VBK[:, fi * 3 + 1, :], U2[fi][:, :], True, False, ["VBK", f"U2{fi}"], [pstk])
                    MM(k, o, VBK[:, fi * 3 + 2, :], VBK[:, fi * 3, :], False, True, ["VBK"], [pstk])
                for fi in range(2):
                    fc = 2 * p + fi
                    for hh in range(2):
                        ps_ = slice(hh * 64, (hh + 1) * 64)
                        STT(k, ST[ps_, fc, :], ST[ps_, fc, :], sc[fi][ps_, 4, c4:c4 + 1], ALU.mult, pst[ps_, fi * 128 + hh * 64:fi * 128 + (hh + 1) * 64], ALU.add,
                            ["ST", f"sc{fi}", pstk], ["ST"])

            for fi in range(2):
                fc = 2 * p + fi
                cs = slice(fc, fc + 1)
                y_, yk_ = Yfm[fi], f"Yfm{fi}"
                ysq, ysqk = hn["ysq"].get()
                A_(k, ysq[:], y_[:], AF.Square, [yk_], [ysqk])
                pm, pmk = PSG.get(); pe2, pe2k = PSG.get()
                MM32(k, C, pm[:, :], "bo64", y_[:], True, True, [yk_], [pmk])
                MM32(k, C, pe2[:, :], "bo64", ysq[:], True, True, [ysqk], [pe2k])
                mean, meank = hn["mean"].get()
                A_(k, mean[:], pm[:, :], AF.Copy, [pmk], [meank], scale=1.0 / 64.0)
                msq, msqk = hn["msq"].get()
                TTo(k, "pool", msq[:], mean[:], mean[:], ALU.mult, [meank], [msqk])
                var, vark = hn["var"].get()
                STT(k, var[:], pe2[:, :], 1.0 / 64.0, ALU.mult, msq[:], ALU.subtract, [pe2k, msqk], [vark])
                A_(k, var[:], var[:], AF.Ln, [vark], [vark], bias=C["epsgn"][:, 0:1], scale=1.0)
                A_(k, var[:], var[:], AF.Exp, [vark], [vark], scale=-0.5)
                yc, yck = hn["yc"].get()
                TTo(k, "pool", yc[:], y_[:], mean[:], ALU.subtract, [yk_, meank], [yck])
                TTo(k, "dve", yc[:], yc[:], var[:], ALU.mult, [yck, vark], [yck])
                TS(k, "pool", yc[:], yc[:], C["gng"][:, cs], ALU.mult, [yck], [yck], s2=C["gnb"][:, cs], op1=ALU.add)
                TTo(k, "dve", yc[:], yc[:], bon[fi][:], ALU.add, [yck, f"bon{fi}"], [yck])
                yo_, yok_ = yo.get()
                TTo(k, "dve", yo_[:], yc[:], szb[fi][:], ALU.mult, [yck, f"szb{fi}"], [yok_])
                k.dma(y_d[:, yoff + fc, t0:t0 + TT], yo_[:], R=[yok_], W=["y1T"], key="yo" + yok_)
        CP(k, "dve", hT[:, :, 0:1], hT[:, :, TT:TT + 1], ["hT"], ["hT"])
    if not fused:
        k.finish(["y1T"])


def vec_cols(v, c0, n):
    return np.ascontiguousarray(v[c0:c0 + 128 * n].reshape(n, 128).T)

def wchunk(w, col0):
    K = w.shape[0]
    return np.ascontiguousarray(w[:, col0:col0 + 128].reshape(K // 128, 128, 128).transpose(1, 0, 2).reshape(128, (K // 128) * 128))

def bd_expand(w4, J):
    out = np.zeros((128, 128), np.float32)
    for n in range(32):
        out[4 * n:4 * n + 4, 4 * n:4 * n + 4] = w4[32 * J + n]
    return out

def common_consts():
    t = np.arange(512)
    c = {
        "ones": np.ones((128, 128), np.float32), "ident": np.eye(128, dtype=np.float32),
        "maskle": np.triu(np.ones((128, 128), np.float32)),
        "eps6": np.full((128, 1), 1e-6, np.float32),
        "rmask": np.tile(np.where(t % 128 == 0, 0.0, 1.0).astype(np.float32), (36, 1)),
        "nmask": np.tile(np.where(t % 128 == 0, -1e30, 0.0).astype(np.float32), (36, 1)),
        "eye4": np.eye(4, dtype=np.float32), "ones4": np.concatenate([np.ones((4, 128), np.float32), np.zeros((124, 128), np.float32)]),
    }
    return c

def host_A(inp, b, s):
    m = dict(common_consts())
    x = inp["x"][b]
    m["xT"] = np.ascontiguousarray(x.T.reshape(16, 128, 4096).transpose(1, 0, 2))
    w_in = inp["l0_w_in"]
    Js = [8 * s + j for j in range(8)] + [8 * (1 - s) + j for j in range(8)]
    cols = [s * 1024 + e * 128 for e in range(8)] + [2048 + s * 1024 + e * 128 for e in range(8)] \
        + [4096 + J * 128 for J in Js] + [6144 + s * 1024 + e * 128 for e in range(8)]
    m["win"] = np.stack([wchunk(w_in, c0) for c0 in cols])
    m["g0pre"] = vec_cols(inp["l0_norm_pre"], 0, 16)
    lc = inp["l0_lru_conv_w"]
    m["lcw"] = np.ascontiguousarray(np.stack([vec_cols(lc[j], s * 1024, 8) for j in range(4)], axis=2).reshape(128, 32))
    m["lcb"] = vec_cols(inp["l0_lru_conv_b"], s * 1024, 8)
    m["lba"] = vec_cols(inp["l0_lru_ba"], s * 1024, 8); m["lbx"] = vec_cols(inp["l0_lru_bx"], s * 1024, 8)
    m["llam"] = vec_cols(inp["l0_lru_lambda"], s * 1024, 8)
    m["lwa"] = np.ascontiguousarray(inp["l0_lru_wa"][8 * s:8 * s + 8].transpose(1, 0, 2).reshape(128, 1024))
    m["lwx"] = np.ascontiguousarray(inp["l0_lru_wx"][8 * s:8 * s + 8].transpose(1, 0, 2).reshape(128, 1024))
    mc = inp["l0_m_conv_w"]; mcb = inp["l0_m_conv_b"]
    m["mcw"] = np.ascontiguousarray(np.stack([np.stack([mc[t, J * 128:(J + 1) * 128] for t in range(4)], axis=1) for J in Js], axis=1).reshape(128, 64))
    m["mcb"] = np.ascontiguousarray(np.stack([mcb[J * 128:(J + 1) * 128] for J in Js], axis=1))
    m["mskip"] = vec_cols(inp["l0_m_skip"], s * 1024, 8); m["mgn"] = vec_cols(inp["l0_m_gn"], s * 1024, 8)
    for nm, key in (("wqbd", "l0_m_wq"), ("wkbd", "l0_m_wk"), ("wvbd", "l0_m_wv")):
        m[nm] = np.ascontiguousarray(np.concatenate([bd_expand(inp[key], J) for J in Js], axis=1))
    wg = np.zeros((128, 48, 128), np.float32)
    wi, wf = inp["l0_m_wi"], inp["l0_m_wf"]
    for qi in range(3):
        for j, J in enumerate(Js):
            r0 = qi * 2048 + J * 128
            wg[:, qi * 16 + j, 0:4] = wi[r0:r0 + 128, 4 * s:4 * s + 4]
            wg[:, qi * 16 + j, 32:36] = wf[r0:r0 + 128, 4 * s:4 * s + 4]
    m["wgate"] = np.ascontiguousarray(wg.reshape(128, 3, 16 * 128).transpose(1, 0, 2))
    gb = np.zeros((36, 1), np.float32)
    gb[0:4, 0] = inp["l0_m_bi"][4 * s:4 * s + 4]; gb[32:36, 0] = inp["l0_m_bf"][4 * s:4 * s + 4]
    m["gbias"] = gb
    return m

def fm(a):
    T, F = a.shape
    return np.ascontiguousarray(a.T.reshape(F // 128, 128, T).transpose(1, 0, 2))

def unfm(a):
    P, C, T = a.shape
    return np.ascontiguousarray(a.transpose(1, 0, 2).reshape(C * 128, T).T)

def wchunks(w):
    return np.stack([wchunk(w, c0) for c0 in range(0, w.shape[1], 128)])

def host_B(yT_full, xT_full, w_out, g_post, half):
    sl = slice(half * 2048, (half + 1) * 2048)
    return {"yT": np.ascontiguousarray(yT_full[:, :, sl]), "xT": np.ascontiguousarray(xT_full[:, :, sl]),
            "wout": wchunks(w_out), "gpost": vec_cols(g_post, 0, 16),
            "ones": np.ones((128, 128), np.float32), "eps6": np.full((128, 1), 1e-6, np.float32)}

def consts_C():
    t = np.arange(512)
    s_ = np.arange(128)[:, None]; t_ = np.arange(128)[None, :]
    bo = np.zeros((128, 128), np.float32); bo[:64, :64] = 1; bo[64:, 64:] = 1
    hm = np.zeros((128, 2), np.float32); hm[:64, 0] = 1; hm[64:, 1] = 1
    return {"ones": np.ones((128, 128), np.float32), "ident": np.eye(128, dtype=np.float32), "bo64": bo, "hmask": hm,
            "mask2": np.concatenate([(t_ > s_), (t_ >= s_)], axis=1).astype(np.float32),
            "maskN": (t_ < s_).astype(np.float32),
            "rmask": np.tile(np.where(t % 128 == 0, 0.0, 1.0).astype(np.float32), (128, 1)),
            "eps6": np.full((128, 1), 1e-6, np.float32), "epsgn": np.full((128, 1), 64e-5, np.float32),
            "tiny": np.full((128, 1), 1e-24, np.float32)}

def host_C(inp, xT_full, s):
    m = consts_C()
    m["xT"] = xT_full
    w_in = inp["l1_w_in"]
    cols = []
    for p in range(8):
        for fi in range(2):
            fc = 2 * p + fi
            for q in range(4):
                cols.append(q * 4096 + s * 2048 + fc * 128)
    m["win"] = np.stack([wchunk(w_in, c0) for c0 in cols])
    m["g1pre"] = vec_cols(inp["l1_norm_pre"], 0, 16)
    mu = inp["l1_mu_rkv"]
    m["murkv"] = np.ascontiguousarray(np.stack([mu[q * 4096 + s * 2048 + fc * 128: q * 4096 + s * 2048 + fc * 128 + 128] for fc in range(16) for q in range(3)], axis=1))
    m["muw"] = vec_cols(inp["l1_mu_w"], 0, 16); m["mua"] = vec_cols(inp["l1_mu_a"], 0, 16)
    for nm, key in (("w0", "l1_w0"), ("a0", "l1_a0"), ("kk_", "l1_k_k"), ("ka", "l1_k_a"), ("gng", "l1_gn_g"), ("gnb", "l1_gn_b")):
        m[nm] = vec_cols(inp[key], s * 2048, 16)
    m["rk"] = vec_cols(inp["l1_r_k"].reshape(-1), s * 2048, 16)
    for nm, key in (("w1", "l1_w1"), ("a1", "l1_a1")):
        m[nm] = np.ascontiguousarray(inp[key].reshape(16, 128, 96).transpose(1, 0, 2).reshape(128, 16 * 96))
    for nm, key in (("w2", "l1_w2"), ("a2", "l1_a2")):
        m[nm] = np.ascontiguousarray(inp[key][:, s * 2048:(s + 1) * 2048])
    return m


def host_F(inp, b):
    m = {"xT": fm(inp["x"][b])}
    for s in range(2):
        for n, v in host_A(inp, b, s).items():
            if n != "xT":
                m[f"a{s}_{n}"] = v
        for n, v in host_C(inp, None, s).items():
            if n != "xT":
                m[f"c{s}_{n}"] = v
    for pfx, wk, gk in (("b0_", "l0_w_out", "l0_norm_post"), ("b1_", "l1_w_out", "l1_norm_post")):
        m[pfx + "wout"] = wchunks(inp[wk]); m[pfx + "gpost"] = vec_cols(inp[gk], 0, 16)
        m[pfx + "ones"] = np.ones((128, 128), np.float32); m[pfx + "eps6"] = np.full((128, 1), 1e-6, np.float32)
    return m

_CACHE = {}


def _prog(name, builder):
    if name not in _CACHE:
        k = K()
        builder(k)
        k.close()
        _CACHE[name] = k.nc
    return _CACHE[name]


def kernel(**inp):
    inp = {n: np.asarray(v) for n, v in inp.items()}
    cores = list(range(8))
    x = inp["x"]
    xT = [fm(x[b]) for b in range(4)]
    ncA = _prog("A", build_A)
    resA = run_bass_kernel_spmd(ncA, [host_A(inp, c // 2, c % 2) for c in cores], core_ids=cores).results
    yT = []
    for b in range(4):
        full = np.empty((128, 32, 4096), dtype=np.asarray(resA[0]["y0T"]).dtype)
        for s in range(2):
            y = np.asarray(resA[2 * b + s]["y0T"])
            full[:, s * 8:(s + 1) * 8] = y[:, 0:8]
            full[:, 16 + s * 8:16 + (s + 1) * 8] = y[:, 8:16]
        yT.append(full)
    ncB = _prog("B", build_B)
    resB = run_bass_kernel_spmd(ncB, [host_B(yT[c // 2], xT[c // 2], inp["l0_w_out"], inp["l0_norm_post"], c % 2) for c in cores], core_ids=cores).results
    x1T = [np.concatenate([np.asarray(resB[2 * b]["xoT"]), np.asarray(resB[2 * b + 1]["xoT"])], axis=2) for b in range(4)]
    ncC = _prog("C", build_C)
    resC = run_bass_kernel_spmd(ncC, [host_C(inp, x1T[c // 2], c % 2) for c in cores], core_ids=cores).results
    y1T = [np.concatenate([np.asarray(resC[2 * b]["y1T"]), np.asarray(resC[2 * b + 1]["y1T"])], axis=1) for b in range(4)]
    resD = run_bass_kernel_spmd(ncB, [host_B(y1T[c // 2], x1T[c // 2], inp["l1_w_out"], inp["l1_norm_post"], c % 2) for c in cores], core_ids=cores).results
    out = np.empty((4, 4096, 2048), np.float32)
    for b in range(4):
        for h in range(2):
            out[b, h * 2048:(h + 1) * 2048] = unfm(np.asarray(resD[2 * b + h]["xoT"]))
    return out
```
